# Optimizing a Trainium2 kernel written in Bass

```python
import math
import jax, jax.numpy as jnp
from jax import lax
import numpy as np

D_MODEL = 2048
BATCH = 1
SEQ = 8192
DEPTH = 4

N_MIXERS = 3
HEAD_DIM = 128
MOBA_HEADS = D_MODEL // HEAD_DIM
MOBA_BLOCK = 256
MOBA_TOP_K = 3
MOBA_QUERY_CHUNK = 64
CONV_WIDTH = 3
DIL_GROUPS = ((128, 1), (512, 4), (2048, 16))
DIL_HEADS = D_MODEL // HEAD_DIM
ROT_DIM = HEAD_DIM // 4
ROPE_THETA = 500000.0
D_FF = 4 * D_MODEL
RMS_EPS = 1e-5
NEG_INF = -1e30

kernel_name = 'hybrid_moba_shortconv_dilated_trunk'


def rmsnorm(x, g):
    x32 = x.astype(jnp.float32)
    y = x32 * lax.rsqrt(jnp.mean(x32 * x32, axis=-1, keepdims=True) + RMS_EPS)
    return (y * g.astype(jnp.float32)).astype(x.dtype)


def partial_rope(t, positions):
    half = ROT_DIM // 2
    inv_freq = ROPE_THETA ** (-jnp.arange(0, ROT_DIM, 2, dtype=jnp.float32) / ROT_DIM)
    ang = positions.astype(jnp.float32)[:, :, None, None] * inv_freq
    cos, sin = jnp.cos(ang), jnp.sin(ang)
    t32 = t.astype(jnp.float32)
    x1, x2 = t32[..., :half], t32[..., half:ROT_DIM]
    rot = jnp.concatenate([x1 * cos - x2 * sin, x2 * cos + x1 * sin], axis=-1)
    return jnp.concatenate([rot.astype(t.dtype), t[..., ROT_DIM:]], axis=-1)


def gather_blocks(blocks, idx):
    return jax.vmap(jax.vmap(lambda bl, i: bl[i]))(blocks, idx)


def moba_attention(xn, w_qkv, w_o, positions):
    B, S, _ = xn.shape
    H, hd, blk, qc = MOBA_HEADS, HEAD_DIM, MOBA_BLOCK, MOBA_QUERY_CHUNK
    qkv = (xn @ w_qkv).reshape(B, S, 3, H, hd)
    q = partial_rope(qkv[:, :, 0], positions)
    k = partial_rope(qkv[:, :, 1], positions)
    v = qkv[:, :, 2]
    sp = -(-S // blk) * blk
    pad = ((0, 0), (0, sp - S), (0, 0), (0, 0))
    q, k, v = [jnp.pad(t, pad).transpose(0, 2, 1, 3) for t in (q, k, v)]
    nb = sp // blk
    kb = k.reshape(B, H, nb, blk, hd)
    vb = v.reshape(B, H, nb, blk, hd)
    k_mean = jnp.mean(kb.astype(jnp.float32), axis=3)
    gate = jnp.einsum('bhsd,bhnd->bhsn', q.astype(jnp.float32), k_mean)
    q_block = jnp.arange(sp) // blk
    fully_past = jnp.arange(nb)[None, :] < q_block[:, None]
    gate = jnp.where(fully_past, gate, NEG_INF)
    n_sel = min(MOBA_TOP_K, nb)
    _, sel = lax.top_k(gate, n_sel)
    sel_valid = jnp.arange(n_sel)[None, :] < q_block[:, None]
    nc = sp // qc
    q_c = q.reshape(B, H, nc, qc, hd).transpose(2, 0, 1, 3, 4)
    sel_c = sel.reshape(B, H, nc, qc, n_sel).transpose(2, 0, 1, 3, 4)
    valid_c = sel_valid.reshape(nc, qc, n_sel)
    scale = HEAD_DIM ** -0.5

    def chunk(args):
        q_i, sel_i, valid_i, c = args
        b = (c * qc) // blk
        k_own = lax.dynamic_index_in_dim(kb, b, axis=2, keepdims=False)
        v_own = lax.dynamic_index_in_dim(vb, b, axis=2, keepdims=False)
        k_sel = gather_blocks(kb, sel_i)
        v_sel = gather_blocks(vb, sel_i)
        s_sel = jnp.einsum('bhqd,bhqnkd->bhqnk', q_i, k_sel).astype(jnp.float32) * scale
        s_sel = jnp.where(valid_i[None, None, :, :, None], s_sel, NEG_INF)
        s_own = jnp.einsum('bhqd,bhkd->bhqk', q_i, k_own).astype(jnp.float32) * scale
        qpos = c * qc + jnp.arange(qc)
        kpos = b * blk + jnp.arange(blk)
        s_own = jnp.where(kpos[None, :] <= qpos[:, None], s_own, NEG_INF)
        logits = jnp.concatenate([s_sel.reshape(B, H, qc, n_sel * blk), s_own], axis=-1)
        p = jax.nn.softmax(logits, axis=-1).astype(v.dtype)
        p_sel = p[..., :n_sel * blk].reshape(B, H, qc, n_sel, blk)
        p_own = p[..., n_sel * blk:]
        return (jnp.einsum('bhqnk,bhqnkd->bhqd', p_sel, v_sel)
                + jnp.einsum('bhqk,bhkd->bhqd', p_own, v_own))

    o = lax.map(chunk, (q_c, sel_c, valid_c, jnp.arange(nc)))
    o = o.transpose(1, 0, 3, 2, 4).reshape(B, sp, H * hd)[:, :S]
    return o @ w_o


def short_conv_mixer(xn, w_in, conv_w, w_out):
    D = xn.shape[-1]
    b_gate, c_gate, u = jnp.split(xn @ w_in, 3, axis=-1)
    z = c_gate * u
    conv = lax.conv_general_dilated(
        z, conv_w[:, None, :].astype(z.dtype), window_strides=(1,),
        padding=((CONV_WIDTH - 1, 0),), dimension_numbers=('NWC', 'WIO', 'NWC'),
        feature_group_count=D)
    return (b_gate * conv) @ w_out


def dilated_group(q, k, v, window, dilation):
    B, H, S, hd = q.shape
    band = window // dilation
    L = S // dilation
    lp = -(-L // band) * band
    nbl = lp // band
    scale = HEAD_DIM ** -0.5

    def to_blocks(t):
        t = t.reshape(B, H, L, dilation, hd).transpose(0, 1, 3, 2, 4)
        t = jnp.pad(t, ((0, 0), (0, 0), (0, 0), (0, lp - L), (0, 0)))
        return t.reshape(B, H, dilation, nbl, band, hd)

    def with_prev(t):
        prev = jnp.concatenate([jnp.zeros_like(t[:, :, :, :1]), t[:, :, :, :-1]], axis=3)
        return jnp.concatenate([prev, t], axis=4)

    qs = to_blocks(q)
    kband = with_prev(to_blocks(k))
    vband = with_prev(to_blocks(v))
    s = jnp.einsum('bhrnqd,bhrnkd->bhrnqk', qs, kband).astype(jnp.float32) * scale
    j = jnp.arange(nbl)[:, None, None]
    qi = j * band + jnp.arange(band)[None, :, None]
    ki = (j - 1) * band + jnp.arange(2 * band)[None, None, :]
    diff = qi - ki
    valid = (diff >= 0) & (diff <= band) & (ki >= 0)
    s = jnp.where(valid, s, NEG_INF)
    m = jnp.max(s, axis=-1, keepdims=True)
    e = jnp.exp(s - m)
    l = jnp.sum(e, axis=-1, keepdims=True)
    o = jnp.einsum('bhrnqk,bhrnkd->bhrnqd', e, vband.astype(jnp.float32)) / l
    lse = (m + jnp.log(l))[..., 0]

    def from_blocks(t):
        t = t.reshape(B, H, dilation, lp, *t.shape[5:])[:, :, :, :L]
        t = jnp.moveaxis(t, 2, 3)
        return t.reshape(B, H, S, *t.shape[4:])

    return from_blocks(o), from_blocks(lse)


def dilated_mixer(xn, w_qkv, w_o, positions):
    B, S, _ = xn.shape
    G = len(DIL_GROUPS)
    qkv = (xn @ w_qkv).reshape(B, S, G, 3, DIL_HEADS, HEAD_DIM)
    outs, lses = [], []
    for g, (window, dilation) in enumerate(DIL_GROUPS):
        q = partial_rope(qkv[:, :, g, 0], positions).transpose(0, 2, 1, 3)
        k = partial_rope(qkv[:, :, g, 1], positions).transpose(0, 2, 1, 3)
        v = qkv[:, :, g, 2].transpose(0, 2, 1, 3)
        o, lse = dilated_group(q, k, v, window, dilation)
        outs.append(o)
        lses.append(lse)
    alpha = jax.nn.softmax(jnp.stack(lses), axis=0)
    o = jnp.einsum('gbhs,gbhsd->bhsd', alpha, jnp.stack(outs)).astype(xn.dtype)
    o = o.transpose(0, 2, 1, 3).reshape(B, S, DIL_HEADS * HEAD_DIM)
    return o @ w_o


def sq_relu_mlp(xn, w_up, w_down):
    h = jax.nn.relu(xn @ w_up)
    return (h * h) @ w_down


def setup_inputs(seed: int = 0) -> dict:
    key = jax.random.key(seed)
    ks = jax.random.split(key, 16)
    n_a = len(range(0, DEPTH, N_MIXERS))
    n_b = len(range(1, DEPTH, N_MIXERS))
    n_c = len(range(2, DEPTH, N_MIXERS))

    def w(k, shape, fan_in):
        return jax.random.normal(k, shape, jnp.float32) * (fan_in ** -0.5)

    def gain(k, shape):
        return 1.0 + 0.01 * jax.random.normal(k, shape, jnp.float32)

    moba_width = MOBA_HEADS * HEAD_DIM
    dil_width = DIL_HEADS * HEAD_DIM
    G = len(DIL_GROUPS)
    return {
        'x': jax.random.normal(ks[0], (BATCH, SEQ, D_MODEL), jnp.float32),
        'positions': jnp.broadcast_to(jnp.arange(SEQ, dtype=jnp.int32), (BATCH, SEQ)),
        'norm_mix': gain(ks[1], (DEPTH, D_MODEL)),
        'norm_mlp': gain(ks[2], (DEPTH, D_MODEL)),
        'norm_final': gain(ks[3], (D_MODEL,)),
        'mlp_w_up': w(ks[4], (DEPTH, D_MODEL, D_FF), D_MODEL),
        'mlp_w_down': w(ks[5], (DEPTH, D_FF, D_MODEL), D_FF),
        'moba_w_qkv': w(ks[6], (n_a, D_MODEL, 3 * moba_width), D_MODEL),
        'moba_w_o': w(ks[7], (n_a, moba_width, D_MODEL), moba_width),
        'conv_w_in': w(ks[8], (n_b, D_MODEL, 3 * D_MODEL), D_MODEL),
        'conv_w': w(ks[9], (n_b, CONV_WIDTH, D_MODEL), CONV_WIDTH),
        'conv_w_out': w(ks[10], (n_b, D_MODEL, D_MODEL), D_MODEL),
        'dil_w_qkv': w(ks[11], (n_c, D_MODEL, G * 3 * dil_width), D_MODEL),
        'dil_w_o': w(ks[12], (n_c, dil_width, D_MODEL), dil_width),
    }


def reference(x, positions, norm_mix, norm_mlp, norm_final, mlp_w_up, mlp_w_down,
              moba_w_qkv, moba_w_o, conv_w_in, conv_w, conv_w_out, dil_w_qkv, dil_w_o):
    h = x
    for i in range(DEPTH):
        kind, j = i % N_MIXERS, i // N_MIXERS
        hn = rmsnorm(h, norm_mix[i])
        if kind == 0:
            mix = moba_attention(hn, moba_w_qkv[j], moba_w_o[j], positions)
        elif kind == 1:
            mix = short_conv_mixer(hn, conv_w_in[j], conv_w[j], conv_w_out[j])
        else:
            mix = dilated_mixer(hn, dil_w_qkv[j], dil_w_o[j], positions)
        h = h + mix
        h = h + sq_relu_mlp(rmsnorm(h, norm_mlp[i]), mlp_w_up[i], mlp_w_down[i])
    return rmsnorm(h, norm_final)
```

```python
import math
from contextlib import ExitStack

import numpy as np
import ml_dtypes
import concourse.bass as bass
import concourse.mybir as mybir
from concourse.bass_utils import run_bass_kernel_spmd

F32 = mybir.dt.float32
BF16 = mybir.dt.bfloat16
I32 = mybir.dt.int32
ALU = mybir.AluOpType
AF = mybir.ActivationFunctionType
AX = mybir.AxisListType
NPBF = ml_dtypes.bfloat16

NCORES = 8
S = 8192
D = 2048
T = 1024
KC = 16
H = 16
HD = 128
DFF = 8192
EPS = 1e-5
SCALE = HD ** -0.5
NEG = -30000.0
ROT = 32
THETA = 500000.0
DIL = ((128, 1), (512, 4), (2048, 16))

ENGS = ("pe", "act", "dve", "pool", "sp")


PSUM_NAMES = {"psA", "ps_sum", "ps_sw", "sps", "ops", "sums", "gps", "tps"}


def _is_psum(h):
    if isinstance(h, tuple):
        return h[0] in PSUM_NAMES
    return h in PSUM_NAMES


class Op:
    __slots__ = ("eng", "fn", "deps", "dma_key", "inc", "cnt", "gid")

    def __init__(self, eng, fn, deps, dma_key, gid):
        self.eng = eng
        self.fn = fn
        self.deps = deps
        self.dma_key = dma_key
        self.inc = False
        self.cnt = 0
        self.gid = gid


class Prog:
    def __init__(self, nc):
        self.nc = nc
        self.ops = []
        self.eng_ops = {e: [] for e in ENGS}
        self.last_writer = {}
        self.readers = {}
        self.dma_cnt = {}

    def op(self, eng, fn, reads=(), writes=(), dma_key=None):
        deps = set()
        xr = [r for r in reads if _is_psum(r)]
        if xr:
            reads = [r for r in reads if not _is_psum(r)]
            writes = list(writes) + xr
        for r in reads:
            w = self.last_writer.get(r)
            if w is not None:
                deps.add(w)
        for wr in writes:
            w = self.last_writer.get(wr)
            if w is not None:
                deps.add(w)
            for rd in self.readers.get(wr, ()):
                deps.add(rd)
        gid = len(self.ops)
        o = Op(eng, fn, deps, dma_key, gid)
        if dma_key is not None:
            self.dma_cnt[dma_key] = self.dma_cnt.get(dma_key, 0) + 16
            o.cnt = self.dma_cnt[dma_key]
        self.ops.append(o)
        self.eng_ops[eng].append(o)
        for r in reads:
            self.readers.setdefault(r, []).append(gid)
        for wr in writes:
            self.last_writer[wr] = gid
            self.readers[wr] = []
        return gid

    def pe(self, fn, reads=(), writes=()):
        return self.op("pe", fn, reads, writes)

    def act(self, fn, reads=(), writes=()):
        return self.op("act", fn, reads, writes)

    def dve(self, fn, reads=(), writes=()):
        return self.op("dve", fn, reads, writes)

    def pool(self, fn, reads=(), writes=()):
        return self.op("pool", fn, reads, writes)

    def dma(self, eng, key, fn, reads=(), writes=()):
        return self.op(eng, fn, reads, writes, dma_key=key)

    def finalize(self, st, final_wait_keys=()):
        nc = self.nc
        ops = self.ops
        for o in ops:
            for d in o.deps:
                od = ops[d]
                if od.dma_key is None:
                    if od.eng == "pe" and o.eng == "pe" and o.dma_key is None:
                        continue
                    od.inc = True
        for e in ENGS:
            c = 0
            for o in self.eng_ops[e]:
                if o.dma_key is None:
                    if o.inc:
                        c += 1
                    o.cnt = c
        esem = {e: st.enter_context(nc.semaphore("s_" + e)) for e in ENGS}
        dsem = {k: st.enter_context(nc.semaphore("d_%d" % i)) for i, k in enumerate(self.dma_cnt)}
        block = st.enter_context(nc.Block())

        def run(e, eng):
            known = {}
            for o in self.eng_ops[e]:
                need = {}
                for d in o.deps:
                    od = ops[d]
                    if od.dma_key is not None:
                        s = dsem[od.dma_key]
                        key = ("d", od.dma_key)
                    else:
                        if od.eng == "pe" and e == "pe" and o.dma_key is None:
                            continue
                        s = esem[od.eng]
                        key = ("e", od.eng)
                    v = od.cnt
                    if need.get(key, (None, 0))[1] < v:
                        need[key] = (s, v)
                for key, (s, v) in need.items():
                    if known.get(key, 0) >= v:
                        continue
                    eng.wait_ge(s, v)
                    known[key] = v
                ins = o.fn(eng)
                if o.dma_key is not None:
                    ins.then_inc(dsem[o.dma_key], 16)
                elif o.inc:
                    ins.then_inc(esem[e], 1)
            if e == "sp":
                for k in final_wait_keys:
                    eng.wait_ge(dsem[k], self.dma_cnt[k])

        @block.tensor
        def _(eng):
            run("pe", eng)

        @block.scalar
        def _(eng):
            run("act", eng)

        @block.vector
        def _(eng):
            run("dve", eng)

        @block.gpsimd
        def _(eng):
            run("pool", eng)

        @block.sync
        def _(eng):
            run("sp", eng)


class Rot:
    def __init__(self, tiles, name):
        self.tiles = tiles
        self.name = name
        self.i = 0

    def next(self):
        k = self.i % len(self.tiles)
        self.i += 1
        return self.tiles[k], (self.name, k)


class Ctx:
    def __init__(self):
        self.nc = bass.Bass("TRN2", target_bir_lowering=False)
        self.P = Prog(self.nc)
        self.st = ExitStack()
        self.n = 0

    def sb(self, shape, dt, name=None):
        self.n += 1
        return self.st.enter_context(self.nc.sbuf_tensor("s_" + (name or ("sb%d" % self.n)), list(shape), dt))

    def ps(self, shape=(128, 512), dt=F32, name=None):
        self.n += 1
        return self.st.enter_context(self.nc.psum_tensor("p_" + (name or ("ps%d" % self.n)), list(shape), dt))

    def din(self, name, shape, dt):
        return self.nc.dram_tensor(name, list(shape), dt, kind="ExternalInput").ap()

    def dout(self, name, shape, dt):
        return self.nc.dram_tensor(name, list(shape), dt, kind="ExternalOutput").ap()


class WStream:
    def __init__(self, cx, nslots=3):
        self.cx = cx
        self.rot = Rot([cx.sb((128, KC, 512), BF16, "w%d" % i) for i in range(nslots)], "w")

    def load(self, w_ap, row0, pieces):
        tile, key = self.rot.next()
        P = self.cx.P
        for (dc, sc, n) in pieces:
            src = w_ap[row0:row0 + KC * 128, sc:sc + n].rearrange("(c p) n -> p c n", p=128)
            P.dma("pool", key, lambda e, tile=tile, dc=dc, n=n, src=src: e.dma_start(out=tile[:, :, dc:dc + n], in_=src),
                  writes=[key])
        return tile, key


def emit_load_hT(cx, hT, h_dram, eng="sp"):
    P = cx.P
    for tg in range(2):
        P.dma(eng, ("ldh", tg), lambda e, tg=tg: e.dma_start(out=hT[:, :, tg * 512:(tg + 1) * 512], in_=h_dram[:, :, tg * 512:(tg + 1) * 512]),
              writes=[("hT", c, tg) for c in range(KC)])


def emit_norm(cx, hT, gT, gcol0, xnT, ones_bf, sq_rot, ps_sum, rstd, tmp):
    P = cx.P
    for tg in range(2):
        sl = slice(tg * 512, (tg + 1) * 512)
        for c in range(KC):
            sq, sqk = sq_rot.next()
            P.act(lambda e, sq=sq, c=c, sl=sl: e.activation(out=sq[:], in_=hT[:, c, sl], func=AF.Square),
                  reads=[("hT", c, tg)], writes=[sqk])
            P.pe(lambda e, sq=sq, c=c: e.matmul(ps_sum[:], ones_bf[:], sq[:], start=(c == 0), stop=(c == KC - 1)),
                 reads=[sqk, "ones"], writes=["ps_sum"])
        P.dve(lambda e: e.tensor_scalar(out=tmp[:], in0=ps_sum[:], scalar1=1.0 / D, scalar2=EPS, op0=ALU.mult, op1=ALU.add),
              reads=["ps_sum"], writes=["ntmp"])
        P.act(lambda e: e.activation(out=tmp[:], in_=tmp[:], func=AF.Sqrt), reads=["ntmp"], writes=["ntmp"])
        P.dve(lambda e, sl=sl: e.reciprocal(out=rstd[:, sl], in_=tmp[:]), reads=["ntmp"], writes=[("rstd", tg)])
        for c in range(KC):
            P.dve(lambda e, c=c, sl=sl: e.scalar_tensor_tensor(out=xnT[:, c, sl], in0=hT[:, c, sl], scalar=gT[:, gcol0 + c:gcol0 + c + 1],
                                                               in1=rstd[:, sl], op0=ALU.mult, op1=ALU.mult),
                  reads=[("hT", c, tg), ("rstd", tg), "gT"], writes=[("xnT", c, tg)])


def emit_fm_chunk(cx, ps, pskey, wtile, wkey, wcol, xT, xname, tg, kc=KC):
    P = cx.P
    sl = slice(tg * 512, (tg + 1) * 512)
    for k in range(kc):
        P.pe(lambda e, k=k: e.matmul(ps[:], wtile[:, k, wcol:wcol + 128], xT[:, k, sl], start=(k == 0), stop=(k == kc - 1)),
             reads=[wkey, (xname, k, tg)], writes=[pskey])


def emit_rope_tables(cx, pos_dram, rc, cosT, sinT):
    P = cx.P
    posi = cx.sb((32, T), I32, "posi")
    ang = cx.sb((32, T), F32, "ang")
    r1 = cx.sb((32, T), F32, "r1")
    P.dma("sp", "pos", lambda e: e.dma_start(out=posi[:], in_=pos_dram), writes=["posi"])
    P.dve(lambda e: e.tensor_copy(out=ang[:], in_=posi[:]), reads=["posi"], writes=["ang"])
    P.dve(lambda e: e.tensor_scalar(out=ang[:], in0=ang[:], scalar1=rc[:, 0:1], scalar2=None, op0=ALU.mult),
          reads=["ang", "rc"], writes=["ang"])
    yi = cx.sb((32, T), I32, "yi")
    yf = cx.sb((32, T), F32, "yf")
    P.dve(lambda e: e.tensor_scalar(out=ang[:], in0=ang[:], scalar1=1.0 / (2 * math.pi), scalar2=None, op0=ALU.mult),
          reads=["ang"], writes=["ang"])
    for (shift, dst, dname, col) in ((0.0, sinT, "sinT", 1), (0.25, cosT, "cosT", 2)):
        P.dve(lambda e, shift=shift: e.tensor_scalar(out=r1[:], in0=ang[:], scalar1=shift, scalar2=None, op0=ALU.add),
              reads=["ang"], writes=["r1"])
        P.dve(lambda e: e.tensor_copy(out=yi[:], in_=r1[:]), reads=["r1"], writes=["yi"])
        P.dve(lambda e: e.tensor_copy(out=yf[:], in_=yi[:]), reads=["yi"], writes=["yf"])
        P.dve(lambda e: e.tensor_tensor(out=r1[:], in0=r1[:], in1=yf[:], op=ALU.subtract), reads=["r1", "yf"], writes=["r1"])
        P.act(lambda e: e.activation(out=r1[:], in_=r1[:], func=AF.Sin, bias=0.0, scale=2 * math.pi), reads=["r1"], writes=["r1"])
        P.dve(lambda e, dst=dst, col=col: e.tensor_scalar(out=dst[:], in0=r1[:], scalar1=rc[:, col:col + 1], scalar2=None, op0=ALU.mult),
              reads=["r1", "rc"], writes=[dname])


def rope_consts():
    rc = np.zeros((32, 4), np.float32)
    inv = (THETA ** (-np.arange(0, ROT, 2, dtype=np.float32) / ROT)).astype(np.float32)
    rc[0:16, 0] = inv
    rc[16:32, 0] = inv
    rc[0:16, 1] = -1.0
    rc[16:32, 1] = 1.0
    rc[:, 2] = 1.0
    rc[:, 3] = -math.pi
    sw = np.zeros((32, 32), np.float32)
    for m in range(32):
        sw[(m + 16) % 32, m] = 1.0
    return rc, sw


DBG = set()


def build_proj(kind):
    cx = Ctx()
    P = cx.P
    ncols = {"moba": 6144, "dil": 18432, "conv": 6144}[kind]
    h_d = cx.din("hT", (128, KC, T), F32)
    g_d = cx.din("g", (128, KC), F32)
    w_d = cx.din("w", (D, ncols), F32)
    ones_d = cx.din("ones", (128, 128), F32)
    hT = cx.sb((128, KC, T), F32, "hT")
    xnT = cx.sb((128, KC, T), BF16, "xnT")
    gT = cx.sb((128, KC), F32, "gT")
    ones_bf = cx.sb((128, 128), BF16, "ones_bf")
    rstd = cx.sb((128, T), F32, "rstd")
    ntmp = cx.sb((128, 512), F32, "ntmp")
    sq_rot = Rot([cx.sb((128, 512), BF16, "sq%d" % i) for i in range(3)], "sq")
    ps_rot = Rot([cx.ps(name="psA%d" % i) for i in range(4)], "psA")
    ps_sum = cx.ps(name="ps_sum")
    ws = WStream(cx, 3)

    emit_load_hT(cx, hT, h_d)
    P.dma("sp", "g", lambda e: e.dma_start(out=gT[:], in_=g_d), writes=["gT"])
    P.dma("pool", "ones", lambda e: e.dma_start(out=ones_bf[:], in_=ones_d), writes=["ones"])
    emit_norm(cx, hT, gT, 0, xnT, ones_bf, sq_rot, ps_sum, rstd, ntmp)

    outs = []
    if kind in ("moba", "dil"):
        G = 1 if kind == "moba" else 3
        pos_d = cx.din("pos", (32, T), I32)
        rc_d = cx.din("rc", (32, 4), F32)
        sw_d = cx.din("sw", (32, 32), F32)
        rc = cx.sb((32, 4), F32, "rc")
        swm = cx.sb((32, 32), BF16, "swm")
        cosT = cx.sb((32, T), F32, "cosT")
        sinT = cx.sb((32, T), F32, "sinT")
        P.dma("sp", "rc", lambda e: e.dma_start(out=rc[:], in_=rc_d), writes=["rc"])
        P.dma("pool", "sw", lambda e: e.dma_start(out=swm[:], in_=sw_d), writes=["swm"])
        emit_rope_tables(cx, pos_d, rc, cosT, sinT)
        ps_sw = cx.ps(name="ps_sw")
        qf_rot = Rot([cx.sb((32, 512), F32, "qf%d" % i) for i in range(2)], "qf")
        qb_rot = Rot([cx.sb((32, 512), BF16, "qb%d" % i) for i in range(2)], "qb")
        t1_rot = Rot([cx.sb((32, 512), F32, "t1%d" % i) for i in range(2)], "t1")
        t2_rot = Rot([cx.sb((32, 512), F32, "t2%d" % i) for i in range(2)], "t2")
        st_rot = Rot([cx.sb((128, T), BF16, "stq%d" % i) for i in range(3)], "stq")
        vst_rot = Rot([cx.sb((128, 512), BF16, "stv%d" % i) for i in range(3)], "stv")
        qk_d = cx.dout("qk", (G, 2, 128, H, T), BF16)
        v_d = cx.dout("v", (G, 128, 8, D), BF16)
        km_d = cx.dout("km", (G, 128, H, 4), F32)
        km = cx.sb((128, H, 4), F32, "km")
        outs += ["qk", "v", "km"]
        nst = 0
        for g in range(G):
            base = g * 3 * D
            for qk in range(2):
                for cg in range(4):
                    c0 = base + qk * D + cg * 512
                    wt, wk = ws.load(w_d, 0, [(0, c0, 512)])
                    for jj in range(4):
                        hd_i = cg * 4 + jj
                        stg, stk = st_rot.next()
                        for tg in range(2):
                            sl = slice(tg * 512, (tg + 1) * 512)
                            ps, pk = ps_rot.next()
                            emit_fm_chunk(cx, ps, pk, wt, wk, jj * 128, xnT, "xnT", tg)
                            P.act(lambda e, ps=ps, stg=stg, sl=sl: e.activation(out=stg[:, sl], in_=ps[:], func=AF.Copy),
                                  reads=[pk], writes=[(stk, "hi", tg), (stk, "lo", tg)])
                            if "norope" in DBG:
                                continue
                            qf, qfk = qf_rot.next()
                            t1, t1k = t1_rot.next()
                            t2, t2k = t2_rot.next()
                            qb, qbk = qb_rot.next()
                            P.dve(lambda e, ps=ps, qf=qf: e.tensor_copy(out=qf[:], in_=ps[0:32, :]), reads=[pk], writes=[qfk])
                            P.act(lambda e, ps=ps, qb=qb: e.activation(out=qb[:], in_=ps[0:32, :], func=AF.Copy), reads=[pk], writes=[qbk])
                            P.pe(lambda e, qb=qb: e.matmul(ps_sw[0:32, :], swm[:], qb[:], start=True, stop=True),
                                 reads=[qbk, "swm"], writes=["ps_sw"])
                            P.dve(lambda e, qf=qf, t1=t1, sl=sl: e.tensor_tensor(out=t1[:], in0=qf[:], in1=cosT[:, sl], op=ALU.mult),
                                  reads=[qfk, "cosT"], writes=[t1k])
                            P.dve(lambda e, t2=t2, sl=sl: e.tensor_tensor(out=t2[:], in0=ps_sw[0:32, :], in1=sinT[:, sl], op=ALU.mult),
                                  reads=["ps_sw", "sinT"], writes=[t2k])
                            P.dve(lambda e, t1=t1, t2=t2, stg=stg, sl=sl: e.tensor_tensor(out=stg[0:32, sl], in0=t1[:], in1=t2[:], op=ALU.add),
                                  reads=[t1k, t2k], writes=[(stk, "lo", tg)])
                        allk = [(stk, a, b) for a in ("hi", "lo") for b in range(2)]
                        if qk == 1 and "nokm" not in DBG:
                            P.dve(lambda e, stg=stg, hd_i=hd_i: e.tensor_reduce(out=km[:, hd_i, :], in_=stg[:].rearrange("p (b k) -> p b k", k=256),
                                                                               axis=AX.X, op=ALU.add),
                                  reads=allk, writes=[("km", hd_i)])
                        P.dma("sp", ("stq", nst % 3), lambda e, stg=stg, g=g, qk=qk, hd_i=hd_i: e.dma_start(out=qk_d[g, qk, :, hd_i, :], in_=stg[:]),
                              reads=allk)
                        nst += 1
            P.dve(lambda e: e.tensor_scalar(out=km[:], in0=km[:], scalar1=1.0 / 256, scalar2=None, op0=ALU.mult),
                  reads=[("km", i) for i in range(H)], writes=[("km", i) for i in range(H)])
            P.dma("sp", "kmo", lambda e, g=g: e.dma_start(out=km_d[g], in_=km[:]), reads=[("km", i) for i in range(H)])
            for cg in range(0 if "nov" in DBG else 4):
                c0 = base + 2 * D + cg * 512
                wt, wk = ws.load(w_d, 0, [(0, c0, 512)])
                for i in range(8):
                    ps, pk = ps_rot.next()
                    for k in range(KC):
                        P.pe(lambda e, ps=ps, wt=wt, k=k, i=i: e.matmul(ps[:], xnT[:, k, i * 128:(i + 1) * 128], wt[:, k, :], start=(k == 0), stop=(k == KC - 1)),
                             reads=[wk, ("xnT", k, i // 4)], writes=[pk])
                    vs, vk = vst_rot.next()
                    P.act(lambda e, ps=ps, vs=vs: e.activation(out=vs[:], in_=ps[:], func=AF.Copy), reads=[pk], writes=[vk])
                    P.dma("sp", vk, lambda e, vs=vs, g=g, i=i, cg=cg: e.dma_start(out=v_d[g, :, i, cg * 512:(cg + 1) * 512], in_=vs[:]), reads=[vk])
        fkeys = [("stq", i) for i in range(3)] + [("stv", i) for i in range(3)] + ["kmo"]
    else:
        b_d = cx.dout("bT", (128, KC, T), F32)
        z_d = cx.dout("zT", (128, KC, T), F32)
        outs += ["bT", "zT"]
        so_rot = Rot([cx.sb((128, 512), F32, "so%d" % i) for i in range(4)], "so")
        ct_rot = Rot([cx.sb((128, 512), F32, "ct%d" % i) for i in range(2)], "ct")
        for cg in range(4):
            wt, wk = ws.load(w_d, 0, [(0, cg * 512, 512)])
            for jj in range(4):
                j = cg * 4 + jj
                for tg in range(2):
                    ps, pk = ps_rot.next()
                    emit_fm_chunk(cx, ps, pk, wt, wk, jj * 128, xnT, "xnT", tg)
                    so, sk = so_rot.next()
                    P.act(lambda e, ps=ps, so=so: e.activation(out=so[:], in_=ps[:], func=AF.Copy), reads=[pk], writes=[sk])
                    P.dma("sp", sk, lambda e, so=so, j=j, tg=tg: e.dma_start(out=b_d[:, j, tg * 512:(tg + 1) * 512], in_=so[:]), reads=[sk])
        for cg in range(8):
            wt, wk = ws.load(w_d, 0, [(0, D + cg * 256, 256), (256, 2 * D + cg * 256, 256)])
            for jj in range(2):
                j = cg * 2 + jj
                for tg in range(2):
                    psc, pkc = ps_rot.next()
                    emit_fm_chunk(cx, psc, pkc, wt, wk, jj * 128, xnT, "xnT", tg)
                    psu, pku = ps_rot.next()
                    emit_fm_chunk(cx, psu, pku, wt, wk, 256 + jj * 128, xnT, "xnT", tg)
                    ct, ck = ct_rot.next()
                    P.act(lambda e, psc=psc, ct=ct: e.activation(out=ct[:], in_=psc[:], func=AF.Copy), reads=[pkc], writes=[ck])
                    so, sk = so_rot.next()
                    P.dve(lambda e, psu=psu, ct=ct, so=so: e.tensor_tensor(out=so[:], in0=psu[:], in1=ct[:], op=ALU.mult),
                          reads=[pku, ck], writes=[sk])
                    P.dma("sp", sk, lambda e, so=so, j=j, tg=tg: e.dma_start(out=z_d[:, j, tg * 512:(tg + 1) * 512], in_=so[:]), reads=[sk])
        fkeys = [("so", i) for i in range(4)]
    P.finalize(cx.st, final_wait_keys=[k for k in fkeys if k in P.dma_cnt])
    cx.st.close()
    return cx.nc, outs


def emit_attn_unit(cx, keytiles, sps_rot, pT_rot, ops, opk, sums, sumk, ones_bf, out_ap, outkey, rec):
    P = cx.P
    n = len(keytiles)
    pend = []

    def emit_s(kt):
        sps, spk = sps_rot.next()
        ex = kt.get("extra", [])
        P.pe(lambda e, sps=sps, kt=kt, ex=ex: e.matmul(sps[:], kt["kT"], kt["q"], start=True, stop=(len(ex) == 0)),
             reads=kt["reads"], writes=[spk])
        for i, (l, r, rd) in enumerate(ex):
            P.pe(lambda e, sps=sps, l=l, r=r, i=i, ex=ex: e.matmul(sps[:], l, r, start=False, stop=(i == len(ex) - 1)),
                 reads=rd, writes=[spk])
        pT, ptk = pT_rot.next()
        b = kt.get("bias")
        if b is None:
            P.act(lambda e, sps=sps, pT=pT: e.activation(out=pT[:], in_=sps[:], func=AF.Exp, scale=SCALE), reads=[spk], writes=[ptk])
        else:
            P.act(lambda e, sps=sps, pT=pT, b=b: e.activation(out=pT[:], in_=sps[:], func=AF.Exp, bias=b, scale=SCALE),
                  reads=[spk] + kt.get("bias_reads", []), writes=[ptk])
        return pT, ptk

    def emit_pv(idx, kt, pT, ptk):
        P.pe(lambda e, kt=kt, pT=pT: e.matmul(ops[:], kt["v"], pT[:], start=(idx == 0), stop=(idx == n - 1)),
             reads=kt["reads"] + [ptk], writes=[opk])
        P.pe(lambda e, pT=pT: e.matmul(sums[:], ones_bf[:], pT[:], start=(idx == 0), stop=(idx == n - 1)),
             reads=[ptk, "ones"], writes=[sumk])

    LOOK = 1
    for i, kt in enumerate(keytiles):
        pend.append((i, kt) + emit_s(kt))
        if len(pend) > LOOK:
            j, ktj, pT, ptk = pend.pop(0)
            emit_pv(j, ktj, pT, ptk)
    for (j, ktj, pT, ptk) in pend:
        emit_pv(j, ktj, pT, ptk)
    P.dve(lambda e: e.reciprocal(out=rec[:], in_=sums[:]), reads=[sumk], writes=["rec"])
    P.dve(lambda e: e.tensor_tensor(out=out_ap, in0=ops[:], in1=rec[:], op=ALU.mult), reads=[opk, "rec"], writes=[outkey])


def build_attn_moba():
    cx = Ctx()
    P = cx.P
    q_d = cx.din("qT", (128, H, T), BF16)
    kl_d = cx.din("kTl", (128, H, T), BF16)
    vl_d = cx.din("vl", (128, H, 8, 128), BF16)
    ka_d = cx.din("kTa", (128, H, S), BF16)
    va_d = cx.din("va", (128, H, 64, 128), BF16)
    km_d = cx.din("kma", (128, H, 32), F32)
    cb_d = cx.din("candb", (128, 4, 32), F32)
    c3_d = cx.din("cand30", (128, 4, 32), F32)
    id_d = cx.din("ident", (128, 128), F32)
    on_d = cx.din("ones", (128, 128), F32)
    E_d = cx.din("E", (32, 32, 128), F32)
    om_d = cx.din("ownmask", (128, 4, 512), F32)
    o_d = cx.dout("oT", (128, H, T), BF16)

    ident_f = cx.sb((128, 128), F32, "ident_f")
    ident_b = cx.sb((128, 128), BF16, "ident_b")
    ones_bf = cx.sb((128, 128), BF16, "ones_bf")
    E = cx.sb((32, 32, 128), BF16, "E")
    ownm = cx.sb((128, 4, 512), BF16, "ownm")
    candb = cx.sb((128, 4, 32), F32, "candb")
    cand30 = cx.sb((128, 4, 32), F32, "cand30")
    kmf = cx.sb((128, H, 32), F32, "kmf")
    kmb = cx.sb((128, H, 32), BF16, "kmb")
    P.dma("sp", "c0", lambda e: e.dma_start(out=ident_f[:], in_=id_d), writes=["ident_f"])
    P.dma("pool", "c1", lambda e: e.dma_start(out=ident_b[:], in_=id_d), writes=["ident_b"])
    P.dma("pool", "c2", lambda e: e.dma_start(out=ones_bf[:], in_=on_d), writes=["ones"])
    P.dma("pool", "c3", lambda e: e.dma_start(out=E[:], in_=E_d), writes=["E"])
    P.dma("pool", "c4", lambda e: e.dma_start(out=ownm[:], in_=om_d), writes=["ownm"])
    P.dma("sp", "c5", lambda e: e.dma_start(out=candb[:], in_=cb_d), writes=["candb"])
    P.dma("sp", "c6", lambda e: e.dma_start(out=cand30[:], in_=c3_d), writes=["cand30"])
    P.dma("sp", "c7", lambda e: e.dma_start(out=kmf[:], in_=km_d), writes=["kmf"])
    P.dve(lambda e: e.tensor_copy(out=kmb[:], in_=kmf[:]), reads=["kmf"], writes=["kmb"])

    NB = 2
    qs = [cx.sb((128, T), BF16, "qs%d" % i) for i in range(NB)]
    kls = [cx.sb((128, T), BF16, "kls%d" % i) for i in range(NB)]
    vls = [cx.sb((128, 8, 128), BF16, "vls%d" % i) for i in range(NB)]
    kas = [cx.sb((128, S), BF16, "kas%d" % i) for i in range(NB)]
    vas = [cx.sb((128, 64, 128), BF16, "vas%d" % i) for i in range(NB)]
    osb = [cx.sb((128, T), BF16, "osb%d" % i) for i in range(NB)]
    sps_rot = Rot([cx.ps(name="sps%d" % i) for i in range(2)], "sps")
    pT_rot = Rot([cx.sb((128, 512), BF16, "pT%d" % i) for i in range(3)], "pT")
    opss = [cx.ps(name="ops%d" % i) for i in range(2)]
    sumss = [cx.ps(name="sums%d" % i) for i in range(2)]
    gps = cx.ps(name="gps")
    tps = cx.ps(name="tps")
    rec = cx.sb((128, 512), F32, "rec")
    gsb = cx.sb((128, 32), F32, "gsb")
    top8 = cx.sb((128, 8), F32, "top8")
    sel = cx.sb((128, 32), F32, "sel")
    sbb = cx.sb((128, 32), F32, "sbb")
    biasT = [cx.sb((32, 512), BF16, "biasT%d" % i) for i in range(2)]

    unit = 0
    for h in range(H):
        b = h % NB
        hk = ("hd", b)
        P.dma("sp", ("lq", b), lambda e, b=b, h=h: e.dma_start(out=qs[b][:], in_=q_d[:, h, :]), writes=[("q", b)])
        P.dma("sp", ("lkl", b), lambda e, b=b, h=h: e.dma_start(out=kls[b][:], in_=kl_d[:, h, :]), writes=[("kl", b)])
        P.dma("sp", ("lvl", b), lambda e, b=b, h=h: e.dma_start(out=vls[b][:], in_=vl_d[:, h]), writes=[("vl", b)])
        P.dma("sp", ("lka", b), lambda e, b=b, h=h: e.dma_start(out=kas[b][:], in_=ka_d[:, h, :]), writes=[("ka", b)])
        P.dma("sp", ("lva", b), lambda e, b=b, h=h: e.dma_start(out=vas[b][:], in_=va_d[:, h]), writes=[("va", b)])
        for i in range(8):
            lb = i // 2
            gq = i // 4
            P.pe(lambda e, b=b, i=i, h=h: e.matmul(gps[:, 0:32], qs[b][:, i * 128:(i + 1) * 128], kmb[:, h, :], start=True, stop=True),
                 reads=[("q", b), "kmb"], writes=["gps"])
            P.dve(lambda e, lb=lb: e.tensor_tensor(out=gsb[:], in0=gps[:, 0:32], in1=candb[:, lb, :], op=ALU.add),
                  reads=["gps", "candb"], writes=["gsb"])
            P.dve(lambda e: e.max(out=top8[:], in_=gsb[:]), reads=["gsb"], writes=["top8"])
            P.dve(lambda e: e.tensor_scalar(out=sel[:], in0=gsb[:], scalar1=top8[:, 2:3], scalar2=None, op0=ALU.is_ge),
                  reads=["gsb", "top8"], writes=["sel"])
            P.dve(lambda e: e.tensor_scalar(out=sel[:], in0=sel[:], scalar1=-NEG, scalar2=NEG, op0=ALU.mult, op1=ALU.add),
                  reads=["sel"], writes=["sel"])
            P.dve(lambda e, lb=lb: e.tensor_tensor(out=sbb[:], in0=sel[:], in1=cand30[:, lb, :], op=ALU.add),
                  reads=["sel", "cand30"], writes=["sbb"])
            P.pe(lambda e, i=i: e.transpose(tps[0:32, (i % 4) * 128:(i % 4 + 1) * 128], sbb[:], ident_f[:]),
                 reads=["sbb", "ident_f"], writes=[("tps", i % 4)])
            if i % 4 == 3:
                P.act(lambda e, gq=gq: e.activation(out=biasT[gq][:], in_=tps[0:32, :], func=AF.Copy),
                      reads=[("tps", j) for j in range(4)], writes=[("biasT", gq)])
        for gq in range(2):
            qap = qs[b][:, gq * 512:(gq + 1) * 512]
            kts = []
            for kt in range(64):
                kts.append(dict(kT=kas[b][:, kt * 128:(kt + 1) * 128], v=vas[b][:, kt, :], q=qap,
                                extra=[(E[:, kt // 2, :], biasT[gq][:], ["E", ("biasT", gq)])],
                                reads=[("ka", b), ("va", b), ("q", b)]))
            for j in range(4):
                lt = gq * 4 + j
                kts.append(dict(kT=kls[b][:, lt * 128:(lt + 1) * 128], v=vls[b][:, lt, :], q=qap,
                                extra=[(ident_b[:], ownm[:, j, :], ["ident_b", "ownm"])],
                                reads=[("kl", b), ("vl", b), ("q", b)]))
            u = unit % 2
            emit_attn_unit(cx, kts, sps_rot, pT_rot, opss[u], ("ops", u), sumss[u], ("sums", u), ones_bf,
                           osb[b][:, gq * 512:(gq + 1) * 512], ("osb", b, gq), rec)
            unit += 1
        P.dma("sp", ("so", b), lambda e, b=b, h=h: e.dma_start(out=o_d[:, h, :], in_=osb[b][:]), reads=[("osb", b, 0), ("osb", b, 1)])
    P.finalize(cx.st, final_wait_keys=[("so", i) for i in range(NB)])
    cx.st.close()
    return cx.nc


def moba_consts():
    ident = np.eye(128, dtype=np.float32)
    ones = np.ones((128, 128), np.float32)
    E = np.zeros((32, 32, 128), np.float32)
    for n in range(32):
        E[n, n, :] = 1.0
    om = np.full((128, 4, 512), NEG, np.float32)
    for j in range(4):
        kpos = j * 128 + np.arange(128)[:, None]
        qpos = np.arange(512)[None, :]
        ok = (kpos // 256 == qpos // 256) & (kpos <= qpos)
        om[:, j, :] = np.where(ok, 0.0, NEG)
    return ident, ones, E, om


def moba_cand(core):
    cb = np.zeros((128, 4, 32), np.float32)
    c3 = np.zeros((128, 4, 32), np.float32)
    for lb in range(4):
        nb = 4 * core + lb
        cb[:, lb, nb:] = -1e30
        c3[:, lb, nb:] = NEG
    return cb, c3


DIL_NT = [(w + 512) // 128 for (w, d) in DIL]
DIL_KLEN = [w + T for (w, d) in DIL]


def build_attn_dil():
    cx = Ctx()
    P = cx.P
    q_d = cx.din("qT", (128, H, 3, T), BF16)
    k_d = [cx.din("kw%d" % g, (128, H, DIL_KLEN[g]), BF16) for g in range(3)]
    v_d = [cx.din("vw%d" % g, (128, H, DIL_KLEN[g] // 128, 128), BF16) for g in range(3)]
    vb_d = cx.din("vbias", (128, 64), F32)
    id_d = cx.din("ident", (128, 128), F32)
    on_d = cx.din("ones", (128, 128), F32)
    dm_d = cx.din("dmask", (128, 33, 512), F32)
    o_d = cx.dout("oT", (128, H, T), BF16)

    ident_b = cx.sb((128, 128), BF16, "ident_b")
    ones_bf = cx.sb((128, 128), BF16, "ones_bf")
    dmask = cx.sb((128, 33, 512), BF16, "dmask")
    vbias = cx.sb((128, 64), F32, "vbias")
    P.dma("pool", "c1", lambda e: e.dma_start(out=ident_b[:], in_=id_d), writes=["ident_b"])
    P.dma("pool", "c2", lambda e: e.dma_start(out=ones_bf[:], in_=on_d), writes=["ones"])
    P.dma("pool", "c4", lambda e: e.dma_start(out=dmask[:], in_=dm_d), writes=["dmask"])
    P.dma("sp", "c5", lambda e: e.dma_start(out=vbias[:], in_=vb_d), writes=["vbias"])
    NB = 2
    qs = [cx.sb((128, 3, T), BF16, "qs%d" % i) for i in range(NB)]
    kws = [[cx.sb((128, DIL_KLEN[g]), BF16, "kw%d_%d" % (g, i)) for g in range(3)] for i in range(NB)]
    vws = [[cx.sb((128, DIL_KLEN[g] // 128, 128), BF16, "vw%d_%d" % (g, i)) for g in range(3)] for i in range(NB)]
    osb = [cx.sb((128, T), BF16, "osb%d" % i) for i in range(NB)]
    sps_rot = Rot([cx.ps(name="sps%d" % i) for i in range(3)], "sps")
    pT_rot = Rot([cx.sb((128, 512), BF16, "pT%d" % i) for i in range(3)], "pT")
    opss = [cx.ps(name="ops%d" % i) for i in range(2)]
    sumss = [cx.ps(name="sums%d" % i) for i in range(2)]
    rec = cx.sb((128, 512), F32, "rec")
    vboff = [0, 9, 21]
    moff = [0, 5, 13]
    unit = 0
    for h in range(H):
        b = h % NB
        P.dma("sp", ("lq", b), lambda e, b=b, h=h: e.dma_start(out=qs[b][:], in_=q_d[:, h]), writes=[("q", b)])
        for g in range(3):
            P.dma("sp", ("lk", g, b), lambda e, b=b, h=h, g=g: e.dma_start(out=kws[b][g][:], in_=k_d[g][:, h, :]), writes=[("k", g, b)])
            P.dma("sp", ("lv", g, b), lambda e, b=b, h=h, g=g: e.dma_start(out=vws[b][g][:], in_=v_d[g][:, h]), writes=[("v", g, b)])
        for gq in range(2):
            kts = []
            for g in range(3):
                for r in range(DIL_NT[g]):
                    kt = 4 * gq + r
                    kts.append(dict(kT=kws[b][g][:, kt * 128:(kt + 1) * 128], v=vws[b][g][:, kt, :],
                                    q=qs[b][:, g, gq * 512:(gq + 1) * 512],
                                    extra=[(ident_b[:], dmask[:, moff[g] + r, :], ["ident_b", "dmask"])],
                                    bias=vbias[:, vboff[g] + kt:vboff[g] + kt + 1], bias_reads=["vbias"],
                                    reads=[("k", g, b), ("v", g, b), ("q", b)]))
            u = unit % 2
            emit_attn_unit(cx, kts, sps_rot, pT_rot, opss[u], ("ops", u), sumss[u], ("sums", u), ones_bf,
                           osb[b][:, gq * 512:(gq + 1) * 512], ("osb", b, gq), rec)
            unit += 1
        P.dma("sp", ("so", b), lambda e, b=b, h=h: e.dma_start(out=o_d[:, h, :], in_=osb[b][:]), reads=[("osb", b, 0), ("osb", b, 1)])
    P.finalize(cx.st, final_wait_keys=[("so", i) for i in range(NB)])
    cx.st.close()
    return cx.nc


def dil_consts():
    dm = np.full((128, 33, 512), NEG, np.float32)
    m = 0
    for (w, d) in DIL:
        for r in range((w + 512) // 128):
            kp = 128 * r + np.arange(128)[:, None]
            qp = np.arange(512)[None, :] + w
            diff = qp - kp
            ok = (diff >= 0) & (diff <= w) & (diff % d == 0)
            dm[:, m, :] = np.where(ok, 0.0, NEG)
            m += 1
    return dm


def dil_vbias(core):
    vb = np.zeros((128, 64), np.float32)
    off = 0
    for (w, d) in DIL:
        nt = (w + T) // 128
        for kt in range(nt):
            tok = core * T - w + kt * 128 + np.arange(128)
            vb[:, off + kt] = np.where(tok >= 0, 0.0, NEG)
        off += nt
    return vb


def build_post(conv=False, final=False):
    cx = Ctx()
    P = cx.P
    h_d = cx.din("hT", (128, KC, T), F32)
    wo_d = cx.din("wo", (D, D), F32)
    g_d = cx.din("g", (128, 2 * KC), F32)
    wu_d = cx.din("wup", (D, DFF), F32)
    wd_d = cx.din("wdn", (DFF, D), F32)
    on_d = cx.din("ones", (128, 128), F32)
    hout_d = cx.dout("hout", (128, KC, T), F32)

    hT = cx.sb((128, KC, T), F32, "hT")
    xT = cx.sb((128, KC, T), BF16, "xT")
    h1T = cx.sb((128, KC, T), BF16, "h1T")
    gT = cx.sb((128, 2 * KC), F32, "gT")
    ones_bf = cx.sb((128, 128), BF16, "ones_bf")
    rstd = cx.sb((128, T), F32, "rstd")
    ntmp = cx.sb((128, 512), F32, "ntmp")
    sq_rot = Rot([cx.sb((128, 512), BF16, "sq%d" % i) for i in range(3)], "sq")
    ps_rot = Rot([cx.ps(name="psA%d" % i) for i in range(6)], "psA")
    ps_sum = cx.ps(name="ps_sum")
    ws = WStream(cx, 3)
    emit_load_hT(cx, hT, h_d)
    P.dma("sp", "g", lambda e: e.dma_start(out=gT[:], in_=g_d), writes=["gT"])
    P.dma("pool", "ones", lambda e: e.dma_start(out=ones_bf[:], in_=on_d), writes=["ones"])

    if not conv:
        o_d = cx.din("oT", (128, KC, T), BF16)
        for tg in range(2):
            P.dma("sp", ("ldo", tg), lambda e, tg=tg: e.dma_start(out=xT[:, :, tg * 512:(tg + 1) * 512], in_=o_d[:, :, tg * 512:(tg + 1) * 512]),
                  writes=[("xT", c, tg) for c in range(KC)])
    else:
        b_d = cx.din("bT", (128, KC, T), F32)
        z_d = cx.din("zTh", (128, KC, T + 2), F32)
        cw_d = cx.din("cw", (128, KC, 3), F32)
        cw = cx.sb((128, KC, 3), F32, "cw")
        P.dma("sp", "cw", lambda e: e.dma_start(out=cw[:], in_=cw_d), writes=["cw"])
        zr = Rot([cx.sb((128, T + 2), F32, "zc%d" % i) for i in range(2)], "zc")
        br = Rot([cx.sb((128, T), F32, "bc%d" % i) for i in range(1)], "bc")
        ar = Rot([cx.sb((128, T), F32, "ac%d" % i) for i in range(1)], "ac")
        for c in range(KC):
            z, zk = zr.next()
            bb, bk = br.next()
            a, ak = ar.next()
            P.dma("sp", zk, lambda e, z=z, c=c: e.dma_start(out=z[:], in_=z_d[:, c, :]), writes=[zk])
            P.dma("sp", bk, lambda e, bb=bb, c=c: e.dma_start(out=bb[:], in_=b_d[:, c, :]), writes=[bk])
            P.dve(lambda e, z=z, a=a, c=c: e.tensor_scalar(out=a[:], in0=z[:, 0:T], scalar1=cw[:, c, 0:1], scalar2=None, op0=ALU.mult),
                  reads=[zk, "cw"], writes=[ak])
            P.dve(lambda e, z=z, a=a, c=c: e.scalar_tensor_tensor(out=a[:], in0=z[:, 1:T + 1], scalar=cw[:, c, 1:2], in1=a[:], op0=ALU.mult, op1=ALU.add),
                  reads=[zk, "cw", ak], writes=[ak])
            P.dve(lambda e, z=z, a=a, c=c: e.scalar_tensor_tensor(out=a[:], in0=z[:, 2:T + 2], scalar=cw[:, c, 2:3], in1=a[:], op0=ALU.mult, op1=ALU.add),
                  reads=[zk, "cw", ak], writes=[ak])
            P.dve(lambda e, a=a, bb=bb, c=c: e.tensor_tensor(out=xT[:, c, :], in0=a[:], in1=bb[:], op=ALU.mult),
                  reads=[ak, bk], writes=[("xT", c, 0), ("xT", c, 1)])

    def resid_evac(ps, pk, j, tg):
        sl = slice(tg * 512, (tg + 1) * 512)
        P.dve(lambda e: e.tensor_tensor(out=hT[:, j, sl], in0=ps[:], in1=hT[:, j, sl], op=ALU.add),
              reads=[pk, ("hT", j, tg)], writes=[("hT", j, tg)])

    for cg in range(4):
        wt, wk = ws.load(wo_d, 0, [(0, cg * 512, 512)])
        for jj in range(4):
            j = cg * 4 + jj
            for tg in range(2):
                ps, pk = ps_rot.next()
                emit_fm_chunk(cx, ps, pk, wt, wk, jj * 128, xT, "xT", tg)
                resid_evac(ps, pk, j, tg)
    emit_norm_named(cx, hT, gT, 0, xT, "xT", ones_bf, sq_rot, ps_sum, rstd, ntmp)
    rl_rot = Rot([cx.sb((128, 512), BF16, "rl%d" % i) for i in range(3)], "rl")
    for qtr in range(4):
        for cg in range(4):
            wt, wk = ws.load(wu_d, 0, [(0, qtr * 2048 + cg * 512, 512)])
            for jj in range(4):
                f = cg * 4 + jj
                for tg in range(2):
                    sl = slice(tg * 512, (tg + 1) * 512)
                    ps, pk = ps_rot.next()
                    emit_fm_chunk(cx, ps, pk, wt, wk, jj * 128, xT, "xT", tg)
                    rl, rk = rl_rot.next()
                    P.act(lambda e, ps=ps, rl=rl: e.activation(out=rl[:], in_=ps[:], func=AF.Relu), reads=[pk], writes=[rk])
                    P.pool(lambda e, rl=rl, f=f, sl=sl: e.tensor_tensor(out=h1T[:, f, sl], in0=rl[:], in1=rl[:], op=ALU.mult),
                           reads=[rk], writes=[("h1T", f, tg)])
        for cg in range(4):
            wt, wk = ws.load(wd_d, qtr * 2048, [(0, cg * 512, 512)])
            for jj in range(4):
                j = cg * 4 + jj
                for tg in range(2):
                    ps, pk = ps_rot.next()
                    emit_fm_chunk(cx, ps, pk, wt, wk, jj * 128, h1T, "h1T", tg)
                    resid_evac(ps, pk, j, tg)
    if final:
        of = cx.sb((128, 512), F32, "of0")
        ofr = Rot([of, cx.sb((128, 512), F32, "of1")], "of")
        for tg in range(2):
            sl = slice(tg * 512, (tg + 1) * 512)
            for c in range(KC):
                sq, sqk = sq_rot.next()
                P.act(lambda e, sq=sq, c=c, sl=sl: e.activation(out=sq[:], in_=hT[:, c, sl], func=AF.Square),
                      reads=[("hT", c, tg)], writes=[sqk])
                P.pe(lambda e, sq=sq, c=c: e.matmul(ps_sum[:], ones_bf[:], sq[:], start=(c == 0), stop=(c == KC - 1)),
                     reads=[sqk, "ones"], writes=["ps_sum"])
            P.dve(lambda e: e.tensor_scalar(out=ntmp[:], in0=ps_sum[:], scalar1=1.0 / D, scalar2=EPS, op0=ALU.mult, op1=ALU.add),
                  reads=["ps_sum"], writes=["ntmp"])
            P.act(lambda e: e.activation(out=ntmp[:], in_=ntmp[:], func=AF.Sqrt), reads=["ntmp"], writes=["ntmp"])
            P.dve(lambda e, sl=sl: e.reciprocal(out=rstd[:, sl], in_=ntmp[:]), reads=["ntmp"], writes=[("rstd", tg)])
            for c in range(KC):
                o, ok = ofr.next()
                P.dve(lambda e, c=c, sl=sl, o=o: e.scalar_tensor_tensor(out=o[:], in0=hT[:, c, sl], scalar=gT[:, KC + c:KC + c + 1],
                                                                        in1=rstd[:, sl], op0=ALU.mult, op1=ALU.mult),
                      reads=[("hT", c, tg), ("rstd", tg), "gT"], writes=[ok])
                P.dma("sp", ok, lambda e, o=o, c=c, sl=sl: e.dma_start(out=hout_d[:, c, sl], in_=o[:]), reads=[ok])
        fk = [("of", 0), ("of", 1)]
    else:
        for tg in range(2):
            P.dma("sp", ("sth", tg), lambda e, tg=tg: e.dma_start(out=hout_d[:, :, tg * 512:(tg + 1) * 512], in_=hT[:, :, tg * 512:(tg + 1) * 512]),
                  reads=[("hT", c, tg) for c in range(KC)])
        fk = [("sth", 0), ("sth", 1)]
    P.finalize(cx.st, final_wait_keys=fk)
    cx.st.close()
    return cx.nc


def emit_norm_named(cx, hT, gT, gcol0, xnT, xname, ones_bf, sq_rot, ps_sum, rstd, tmp):
    P = cx.P
    for tg in range(2):
        sl = slice(tg * 512, (tg + 1) * 512)
        for c in range(KC):
            sq, sqk = sq_rot.next()
            P.act(lambda e, sq=sq, c=c, sl=sl: e.activation(out=sq[:], in_=hT[:, c, sl], func=AF.Square),
                  reads=[("hT", c, tg)], writes=[sqk])
            P.pe(lambda e, sq=sq, c=c: e.matmul(ps_sum[:], ones_bf[:], sq[:], start=(c == 0), stop=(c == KC - 1)),
                 reads=[sqk, "ones"], writes=["ps_sum"])
        P.dve(lambda e: e.tensor_scalar(out=tmp[:], in0=ps_sum[:], scalar1=1.0 / D, scalar2=EPS, op0=ALU.mult, op1=ALU.add),
              reads=["ps_sum"], writes=["ntmp"])
        P.act(lambda e: e.activation(out=tmp[:], in_=tmp[:], func=AF.Sqrt), reads=["ntmp"], writes=["ntmp"])
        P.dve(lambda e, sl=sl: e.reciprocal(out=rstd[:, sl], in_=tmp[:]), reads=["ntmp"], writes=[("rstd", tg)])
        for c in range(KC):
            P.dve(lambda e, c=c, sl=sl: e.scalar_tensor_tensor(out=xnT[:, c, sl], in0=hT[:, c, sl], scalar=gT[:, gcol0 + c:gcol0 + c + 1],
                                                               in1=rstd[:, sl], op0=ALU.mult, op1=ALU.mult),
                  reads=[("hT", c, tg), ("rstd", tg), "gT"], writes=[(xname, c, tg)])


_CACHE = {}
_TRACE = None


def _prog(name, fn, *a):
    key = (name,) + a
    if key not in _CACHE:
        _CACHE[key] = fn(*a)
    return _CACHE[key]


def _run(nc, in_maps):
    in_maps = [{k: np.ascontiguousarray(v) for k, v in m.items()} for m in in_maps]
    res = run_bass_kernel_spmd(nc, in_maps, core_ids=list(range(NCORES)))
    return res.results


def to_fm(x):
    t = x.shape[0]
    return np.ascontiguousarray(x.T.reshape(KC, 128, t).transpose(1, 0, 2))


def gain_fm(g):
    return np.ascontiguousarray(g.reshape(KC, 128).T)


def kernel(x, positions, norm_mix, norm_mlp, norm_final, mlp_w_up, mlp_w_down,
           moba_w_qkv, moba_w_o, conv_w_in, conv_w, conv_w_out, dil_w_qkv, dil_w_o):
    x = np.asarray(x)
    positions = np.asarray(positions)
    C = NCORES
    ones = np.ones((128, 128), np.float32)
    rc, sw = rope_consts()
    ident, _, E, om = moba_consts()
    hT = [to_fm(x[0, c * T:(c + 1) * T]) for c in range(C)]
    pos = [np.ascontiguousarray(np.broadcast_to(positions[0, c * T:(c + 1) * T][None, :], (32, T))).astype(np.int32) for c in range(C)]

    def stage_proj(kind, hT, g, w):
        nc, _ = _prog("proj", build_proj, kind)
        gm = gain_fm(np.asarray(g))
        w = np.ascontiguousarray(np.asarray(w))
        ims = []
        for c in range(C):
            m = {"hT": hT[c], "g": gm, "w": w, "ones": ones}
            if kind != "conv":
                m.update({"pos": pos[c], "rc": rc, "sw": sw})
            ims.append(m)
        return _run(nc, ims)

    def stage_post(hT, extra, wo, gm, gf, wu, wd, conv=False, final=False):
        nc = _prog("post", build_post, conv, final)
        g2 = np.concatenate([gain_fm(np.asarray(gm)), gain_fm(np.asarray(gf))], axis=1)
        ims = []
        for c in range(C):
            m = {"hT": hT[c], "wo": np.asarray(wo), "g": g2, "wup": np.asarray(wu), "wdn": np.asarray(wd), "ones": ones}
            m.update(extra[c])
            ims.append(m)
        r = _run(nc, ims)
        return [r[c]["hout"] for c in range(C)]

    def moba_layer(hT, i, j, final=False):
        r = stage_proj("moba", hT, norm_mix[i], moba_w_qkv[j])
        qT = [r[c]["qk"][0, 0] for c in range(C)]
        kT = [r[c]["qk"][0, 1] for c in range(C)]
        v = [r[c]["v"][0] for c in range(C)]
        kTa = np.ascontiguousarray(np.concatenate(kT, axis=2))
        vl = [np.ascontiguousarray(v[c].reshape(128, 8, H, 128).transpose(0, 2, 1, 3)) for c in range(C)]
        va = np.ascontiguousarray(np.concatenate(vl, axis=2))
        kma = np.ascontiguousarray(np.concatenate([r[c]["km"][0] for c in range(C)], axis=2))
        nc = _prog("attn_moba", build_attn_moba)
        ims = []
        for c in range(C):
            cb, c3 = moba_cand(c)
            ims.append({"qT": qT[c], "kTl": kT[c], "vl": vl[c], "kTa": kTa, "va": va, "kma": kma, "candb": cb, "cand30": c3,
                        "ident": ident, "ones": ones, "E": E, "ownmask": om})
        ro = _run(nc, ims)
        extra = [{"oT": ro[c]["oT"]} for c in range(C)]
        return stage_post(hT, extra, moba_w_o[j], norm_mlp[i], norm_final, mlp_w_up[i], mlp_w_down[i], final=final)

    def conv_layer(hT, i, j):
        r = stage_proj("conv", hT, norm_mix[i], conv_w_in[j])
        cwm = np.ascontiguousarray(np.asarray(conv_w[j]).T.reshape(KC, 128, 3).transpose(1, 0, 2))
        extra = []
        for c in range(C):
            z = r[c]["zT"]
            halo = r[c - 1]["zT"][:, :, T - 2:] if c > 0 else np.zeros((128, KC, 2), np.float32)
            extra.append({"bT": r[c]["bT"], "zTh": np.ascontiguousarray(np.concatenate([halo, z], axis=2)), "cw": cwm})
        return stage_post(hT, extra, conv_w_out[j], norm_mlp[i], norm_final, mlp_w_up[i], mlp_w_down[i], conv=True)

    def dil_layer(hT, i, j):
        r = stage_proj("dil", hT, norm_mix[i], dil_w_qkv[j])
        dm = dil_consts()
        qT = [np.ascontiguousarray(r[c]["qk"][:, 0].transpose(1, 2, 0, 3)) for c in range(C)]
        kws, vws = [[] for _ in range(C)], [[] for _ in range(C)]
        for g, (w, d) in enumerate(DIL):
            kfull = np.concatenate([np.zeros((128, H, w), NPBF)] + [r[c]["qk"][g, 1] for c in range(C)], axis=2)
            vfull = np.concatenate([r[c]["v"][g].reshape(128, 8, H, 128).transpose(0, 2, 1, 3) for c in range(C)], axis=2)
            vfull = np.concatenate([np.zeros((128, H, w // 128, 128), NPBF), vfull], axis=2)
            for c in range(C):
                kws[c].append(np.ascontiguousarray(kfull[:, :, c * T:c * T + w + T]))
                vws[c].append(np.ascontiguousarray(vfull[:, :, c * 8:c * 8 + (w + T) // 128]))
        nc = _prog("attn_dil", build_attn_dil)
        ims = []
        for c in range(C):
            m = {"qT": qT[c], "vbias": dil_vbias(c), "ident": ident, "ones": ones, "dmask": dm}
            for g in range(3):
                m["kw%d" % g] = kws[c][g]
                m["vw%d" % g] = vws[c][g]
            ims.append(m)
        ro = _run(nc, ims)
        extra = [{"oT": ro[c]["oT"]} for c in range(C)]
        return stage_post(hT, extra, dil_w_o[j], norm_mlp[i], norm_final, mlp_w_up[i], mlp_w_down[i])

    hT = moba_layer(hT, 0, 0)
    if _TRACE is not None:
        _TRACE["h0"] = hT
    hT = conv_layer(hT, 1, 0)
    if _TRACE is not None:
        _TRACE["h1"] = hT
    hT = dil_layer(hT, 2, 0)
    if _TRACE is not None:
        _TRACE["h2"] = hT
    hT = moba_layer(hT, 3, 1, final=True)
    out = np.concatenate([hT[c].transpose(2, 1, 0).reshape(T, D) for c in range(C)], axis=0)
    return out[None].astype(np.float32)
```
